# Optimizing a Trainium2 kernel written in Bass

```python
import jax, jax.numpy as jnp
from jax import lax
import numpy as np

D_MODEL = 4096
BATCH = 1
SEQ = 16384
DEPTH = 1

GRID_W = 64
CHUNK = 128
GM_WIDTH = D_MODEL // 2
GM_GROUP_DIM = 128
GM_GROUPS = GM_WIDTH // GM_GROUP_DIM
NA_WIDTH = D_MODEL // 2
NA_HEAD_DIM = 128
NA_HEADS = NA_WIDTH // NA_HEAD_DIM
NA_WIN_H_MAX = 8
NA_WIN_W = 16
D_FF = ((-(-8 * D_MODEL // 3) + 255) // 256) * 256
RMS_EPS = 1e-6
LN_EPS = 1e-5

kernel_name = "hybrid_gmlp_natten_gated_block"


def rms_norm(x, g):
    xf = x.astype(jnp.float32)
    y = xf * lax.rsqrt(jnp.mean(xf * xf, axis=-1, keepdims=True) + RMS_EPS)
    return (y * g.astype(jnp.float32)).astype(x.dtype)


def layer_norm(x, g, b):
    xf = x.astype(jnp.float32)
    mu = jnp.mean(xf, axis=-1, keepdims=True)
    xc = xf - mu
    y = xc * lax.rsqrt(jnp.mean(xc * xc, axis=-1, keepdims=True) + LN_EPS)
    return (y * g.astype(jnp.float32) + b.astype(jnp.float32)).astype(x.dtype)


def gmlp_spatial_gating(u, v, ln_g, ln_b, w_s, b_s):
    B, S, _ = u.shape
    n_chunks = S // CHUNK
    u = jax.nn.gelu(u, approximate=False)
    v = layer_norm(jax.nn.gelu(v, approximate=False), ln_g, ln_b)
    v = v.reshape(B, n_chunks, CHUNK, GM_GROUPS, GM_GROUP_DIM)
    mixed = jnp.einsum("gij,bnjgc->bnigc", w_s, v) + b_s.T[None, None, :, :, None]
    return u * mixed.reshape(B, S, GM_WIDTH)


def neighbourhood_attention(q, k, v, q_gain, k_gain, rpb):
    B, S, _ = q.shape
    rows = S // GRID_W
    kh = min(NA_WIN_H_MAX, rows)
    scale = NA_HEAD_DIM ** -0.5

    def to_grid(t):
        return t.reshape(B, rows, GRID_W, NA_HEADS, NA_HEAD_DIM)

    qg = rms_norm(to_grid(q), q_gain)
    kg = rms_norm(to_grid(k), k_gain)
    vg = to_grid(v)

    cols = np.arange(GRID_W)
    col_start = np.clip(cols - NA_WIN_W // 2, 0, GRID_W - NA_WIN_W)
    col_idx = col_start[:, None] + np.arange(NA_WIN_W)[None, :]
    col_off = col_idx - cols[:, None] + (NA_WIN_W - 1)
    rpb_cols = rpb[:, :, col_off].astype(jnp.float32)

    def one_row(args):
        r, q_row = args
        rs = jnp.clip(r - kh // 2, 0, rows - kh)
        k_band = lax.dynamic_slice_in_dim(kg, rs, kh, axis=1)
        v_band = lax.dynamic_slice_in_dim(vg, rs, kh, axis=1)
        k_win = k_band[:, :, col_idx]
        v_win = v_band[:, :, col_idx]
        s = jnp.einsum("bwhd,biwjhd->bhwij", q_row, k_win).astype(jnp.float32) * scale
        row_off = rs + jnp.arange(kh) - r + (NA_WIN_H_MAX - 1)
        bias = jnp.take(rpb_cols, row_off, axis=1)
        s = s + jnp.transpose(bias, (0, 2, 1, 3))[None]
        p = jax.nn.softmax(s, axis=(-2, -1)).astype(v_win.dtype)
        return jnp.einsum("bhwij,biwjhd->bwhd", p, v_win)

    q_rows = jnp.moveaxis(qg, 1, 0)
    out = lax.map(one_row, (jnp.arange(rows, dtype=jnp.int32), q_rows))
    return jnp.moveaxis(out, 0, 1).reshape(B, S, NA_WIDTH)


def swiglu(h, w_gate, w_up, w_down):
    return (jax.nn.silu(h @ w_gate) * (h @ w_up)) @ w_down


def setup_inputs(seed: int = 0) -> dict:
    key = jax.random.key(seed)
    ks = jax.random.split(key, 17)
    L = DEPTH
    in_cols = 2 * GM_WIDTH + 3 * NA_WIDTH + 2 * D_MODEL

    def nrm(k, shape, scale):
        return jax.random.normal(k, shape, jnp.float32) * scale

    return {
        "x": nrm(ks[0], (BATCH, SEQ, D_MODEL), 1.0),
        "norm1_g": 1.0 + nrm(ks[1], (L, D_MODEL), 0.02),
        "w_in": nrm(ks[2], (L, D_MODEL, in_cols), D_MODEL ** -0.5),
        "gm_ln_g": 1.0 + nrm(ks[3], (L, GM_WIDTH), 0.02),
        "gm_ln_b": nrm(ks[4], (L, GM_WIDTH), 0.02),
        "gm_w_s": nrm(ks[5], (L, GM_GROUPS, CHUNK, CHUNK), CHUNK ** -0.5),
        "gm_b_s": 1.0 + nrm(ks[6], (L, GM_GROUPS, CHUNK), 0.02),
        "q_gain": 1.0 + nrm(ks[7], (L, NA_HEAD_DIM), 0.02),
        "k_gain": 1.0 + nrm(ks[8], (L, NA_HEAD_DIM), 0.02),
        "na_rpb": nrm(ks[9], (L, NA_HEADS, 2 * NA_WIN_H_MAX - 1, 2 * NA_WIN_W - 1), 0.02),
        "w_o_gm": nrm(ks[10], (L, GM_WIDTH, D_MODEL), GM_WIDTH ** -0.5),
        "w_o_na": nrm(ks[11], (L, NA_WIDTH, D_MODEL), NA_WIDTH ** -0.5),
        "w_out": nrm(ks[12], (L, D_MODEL, D_MODEL), D_MODEL ** -0.5),
        "norm2_g": 1.0 + nrm(ks[13], (L, D_MODEL), 0.02),
        "w_ff_gate": nrm(ks[14], (L, D_MODEL, D_FF), D_MODEL ** -0.5),
        "w_ff_up": nrm(ks[15], (L, D_MODEL, D_FF), D_MODEL ** -0.5),
        "w_ff_down": nrm(ks[16], (L, D_FF, D_MODEL), D_FF ** -0.5),
    }


def reference(x, norm1_g, w_in, gm_ln_g, gm_ln_b, gm_w_s, gm_b_s, q_gain, k_gain, na_rpb,
              w_o_gm, w_o_na, w_out, norm2_g, w_ff_gate, w_ff_up, w_ff_down):
    split_at = list(np.cumsum([GM_WIDTH, GM_WIDTH, NA_WIDTH, NA_WIDTH, NA_WIDTH, D_MODEL]))
    h = x
    for layer in range(DEPTH):
        xn = rms_norm(h, norm1_g[layer])
        proj = xn @ w_in[layer]
        u_a, v_a, q, k, v, g_a, g_b = jnp.split(proj, split_at, axis=-1)
        y_a = gmlp_spatial_gating(u_a, v_a, gm_ln_g[layer], gm_ln_b[layer],
                                  gm_w_s[layer], gm_b_s[layer])
        y_b = neighbourhood_attention(q, k, v, q_gain[layer], k_gain[layer], na_rpb[layer])
        merged = (jax.nn.sigmoid(g_a) * (y_a @ w_o_gm[layer])
                  + jax.nn.sigmoid(g_b) * (y_b @ w_o_na[layer]))
        h = h + merged @ w_out[layer]
        h = h + swiglu(rms_norm(h, norm2_g[layer]), w_ff_gate[layer], w_ff_up[layer], w_ff_down[layer])
    return h
```

```python
import contextlib
import numpy as np
import concourse.bass as bass
import concourse.mybir as mybir
from concourse.bass_utils import run_bass_kernel_spmd

F32 = mybir.dt.float32
BF16 = mybir.dt.bfloat16
AF = mybir.ActivationFunctionType
ALU = mybir.AluOpType

GRID_W = 64
RMS_EPS = 1e-6
LN_EPS = 1e-5
NEG = -30000.0
T = 512
KB = {2: (0, 6), 3: (1, 6), 4: (2, 7), 5: (2, 8)}
NM_OFF = {2: 0, 3: 6, 4: 11, 5: 16}
NM_TILES = 22
ENGS = ("pe", "act", "dve", "pool", "sp")


class Cfg:
    def __init__(self, D=4096, SEQ=16384, DFF=11008, NCORES=8):
        self.D, self.SEQ, self.DFF, self.NCORES = D, SEQ, DFF, NCORES
        self.GW = D // 2
        self.NW = D // 2
        self.KC = D // 128
        self.NG = self.GW // 128
        self.NH = self.NW // 128
        self.FC = DFF // 128
        self.IN_COLS = 2 * self.GW + 3 * self.NW + 2 * D
        self.TOK = SEQ // NCORES
        self.NGRP = self.TOK // T
        self.TPC = self.TOK // 128
        self.ROWS = SEQ // GRID_W
        self.U0 = 0
        self.V0 = self.GW
        self.Q0 = 2 * self.GW
        self.K0 = self.Q0 + self.NW
        self.VV0 = self.K0 + self.NW
        self.GA0 = self.VV0 + self.NW
        self.GB0 = self.GA0 + D
        assert self.TOK % T == 0 and D % 256 == 0 and DFF % 128 == 0


class Instr:
    __slots__ = ("eng", "fn", "deps", "same", "is_dma", "dkey", "dcum", "signal", "rank", "epoch", "idx")


class Sched:
    BUCKET = 16384

    def __init__(self):
        self.streams = {e: [] for e in ENGS}
        self.acc = {}
        self.dma_cum = {}
        self.closed = set()
        self.epoch = 0
        self.n = 0
        self.tracked = set()
        self.psum_names = set()
        self.deferred = []
        self.out_dmas = []

    def track(self, name):
        self.tracked.add(name)

    def _rng(self, ap):
        name = ap.tensor.name
        if name not in self.tracked:
            return None
        pstep = ap.ap[0][0]
        esz = mybir.dt.size(ap.dtype)
        off = int(ap.offset) % pstep
        ext = 1
        for st, cnt in ap.ap[1:]:
            ext += st * (cnt - 1)
        return name, off * esz, (off + ext) * esz

    def _buckets(self, name, lo, hi):
        b0 = lo // self.BUCKET
        b1 = (hi - 1) // self.BUCKET
        for b in range(b0, b1 + 1):
            yield (name, b), max(lo, b * self.BUCKET), min(hi, (b + 1) * self.BUCKET)

    def op(self, eng, fn, reads=(), writes=(), dkey=None, closed=False):
        ins = Instr()
        ins.eng = eng
        ins.fn = fn
        ins.is_dma = dkey is not None
        ins.dkey = dkey
        ins.signal = False
        ins.rank = 0
        ins.epoch = self.epoch
        ins.idx = self.n
        self.n += 1
        if ins.is_dma:
            c = self.dma_cum.get(dkey, 0) + 16
            self.dma_cum[dkey] = c
            ins.dcum = c
            if closed:
                self.closed.add(dkey)
        deps = {}
        dmadeps = {}
        same = None
        accs = []
        for a, isw in [(x, False) for x in reads] + [(x, True) for x in writes]:
            r = self._rng(a)
            if r is None:
                continue
            name, lo, hi = r
            if name in self.psum_names:
                lo = (lo // 2048) * 2048
                hi = ((hi + 2047) // 2048) * 2048
                accs.append((name, lo, hi, True, isw))
            else:
                accs.append((name, lo, hi, isw, isw))

        def add_dep(p, kind):
            nonlocal same
            if p is ins:
                return
            if p.is_dma:
                if dmadeps.get(p.dkey, 0) < p.dcum:
                    dmadeps[p.dkey] = p.dcum
            elif p.eng == eng and not ins.is_dma:
                if eng != "pe" and kind != "war":
                    if same is None or same.idx < p.idx:
                        same = p
            elif p.eng == eng and ins.is_dma:
                if same is None or same.idx < p.idx:
                    same = p
            else:
                q = deps.get(p.eng)
                if q is None or q.idx < p.idx:
                    deps[p.eng] = p

        for name, lo0, hi0, cw, rw in accs:
            for key, lo, hi in self._buckets(name, lo0, hi0):
                for e in self.acc.get(key, ()):
                    if e[0] < hi and lo < e[1] and (cw or e[3]):
                        if e[4]:
                            add_dep(e[2], "waw" if rw else "raw")
                        else:
                            add_dep(e[2], "war")
        for name, lo0, hi0, cw, rw in accs:
            for key, lo, hi in self._buckets(name, lo0, hi0):
                lst = self.acc.get(key)
                if lst is None:
                    lst = self.acc[key] = []
                if cw:
                    lst[:] = [e for e in lst if e[2] is ins or not (lo <= e[0] and e[1] <= hi)]
                    lst.append([lo, hi, ins, True, rw])
                else:
                    for e in lst:
                        if (not e[3]) and e[0] == lo and e[1] == hi and e[2].eng == eng \
                                and (not e[2].is_dma) and (not ins.is_dma):
                            e[2] = ins
                            break
                    else:
                        lst.append([lo, hi, ins, False, False])
        ins.deps = (list(deps.values()), dmadeps)
        ins.same = same
        self.streams[eng].append(ins)
        return ins

    def dma(self, queue, out, in_, key, closed=False):
        return self.op(queue, lambda e: e.dma_start(out=out, in_=in_),
                       reads=[in_], writes=[out], dkey=key, closed=closed)

    def defer(self, fn):
        self.deferred.append(fn)

    def take_deferred(self):
        d = self.deferred
        self.deferred = []
        return d

    def flush(self):
        d = self.deferred
        self.deferred = []
        for fn in d:
            fn()

    def prepare(self, nc, stack):
        for s in self.streams.values():
            for ins in s:
                for p in ins.deps[0]:
                    p.signal = True
                if ins.same is not None:
                    ins.same.signal = True
        engsem = {}
        for eng, s in self.streams.items():
            cnt = {}
            for ins in s:
                if ins.signal:
                    k = (eng, ins.epoch)
                    cnt[k] = cnt.get(k, 0) + 1
                    ins.rank = cnt[k]
                    if k not in engsem:
                        engsem[k] = stack.enter_context(nc.semaphore("s_%s_%d" % k))
        dmasem = {}
        for i, k in enumerate(self.dma_cum):
            dmasem[k] = stack.enter_context(nc.semaphore("d_%d" % i))
        self.n_sems = len(engsem) + len(dmasem)
        self.engsem, self.dmasem = engsem, dmasem

    def emit(self, block):
        engsem, dmasem = self.engsem, self.dmasem

        def run(eng, e):
            waited = {}

            def w(sem, val):
                if waited.get(sem, 0) >= val:
                    return
                waited[sem] = val
                e.wait_ge(sem, val)

            for ins in self.streams[eng]:
                prods, dmadeps = ins.deps
                for p in prods:
                    w(engsem[(p.eng, p.epoch)], p.rank)
                if ins.same is not None:
                    p = ins.same
                    w(engsem[(p.eng, p.epoch)], p.rank)
                for k, c in dmadeps.items():
                    if k in self.closed:
                        c = self.dma_cum[k]
                    w(dmasem[k], c)
                bi = ins.fn(e)
                if ins.is_dma:
                    bi.then_inc(dmasem[ins.dkey], 16)
                elif ins.signal:
                    bi.then_inc(engsem[(eng, ins.epoch)], 1)
            if eng == "sp":
                for k in self.out_dmas:
                    w(dmasem[k], self.dma_cum[k])

        @block.tensor
        def _(e):
            run("pe", e)

        @block.scalar
        def _(e):
            run("act", e)

        @block.vector
        def _(e):
            run("dve", e)

        @block.gpsimd
        def _(e):
            run("pool", e)

        @block.sync
        def _(e):
            run("sp", e)


class Banks:
    def __init__(self, aps):
        self.aps = aps
        self.busy = [False] * len(aps)
        self.stamp = list(range(len(aps)))
        self.clock = len(aps)

    def alloc(self):
        free = [j for j in range(len(self.aps)) if not self.busy[j]]
        if not free:
            raise RuntimeError("all PSUM banks busy")
        j = min(free, key=lambda k: self.stamp[k])
        self.busy[j] = True
        return j, self.aps[j]

    def release(self, j):
        assert self.busy[j]
        self.busy[j] = False
        self.stamp[j] = self.clock
        self.clock += 1


class Ring:
    def __init__(self, items):
        self.items = items
        self.i = 0

    def next(self):
        it = self.items[self.i % len(self.items)]
        self.i += 1
        return it


def bc_last(ap, n):
    a = [list(x) for x in ap.ap]
    return bass.AP(ap.tensor, ap.offset, a + [[0, n]])


def build(cfg):
    D, KC, NG, NH, GW, NW, DFF = cfg.D, cfg.KC, cfg.NG, cfg.NH, cfg.GW, cfg.NW, cfg.DFF
    nc = bass.Bass("TRN2", target_bir_lowering=False)
    S = Sched()

    def dram_in(name, shape):
        return nc.dram_tensor(name, list(shape), F32, kind="ExternalInput").ap()

    x_d = dram_in("x", [cfg.TOK + 512, D])
    tiny = getattr(cfg, "TINYW", False)
    wshape = lambda sh: [128, 128] if tiny else sh
    w_in = dram_in("w_in", wshape([D, cfg.IN_COLS]))
    w_oa = dram_in("w_o_gm", wshape([GW, D]))
    w_ob = dram_in("w_o_na", wshape([NW, D]))
    w_out = dram_in("w_out", wshape([D, D]))
    w_g = dram_in("w_ff_gate", wshape([D, DFF]))
    w_u = dram_in("w_ff_up", wshape([D, DFF]))
    w_d = dram_in("w_ff_down", wshape([DFF, D]))
    NV = 2 * KC + NG + 2
    vecs_d = dram_in("vecs", [128, NV])
    lnb_d = dram_in("lnb_row", [1, GW])
    bs_d = dram_in("bs_row", [1, GW])
    wsT_d = dram_in("wsT", [128, GW])
    bias_d = dram_in("biasT", [NH * 128, 7 * 128])
    nm_d = dram_in("negmask", [cfg.NGRP * 128, NM_TILES * 128])
    out_d = nc.dram_tensor("out", [cfg.TOK, D], F32, kind="ExternalOutput").ap()

    H1_SZ = 16 * D
    B1_SZ = 8 * D
    ATT_TMP = 2 * 1792 + NM_TILES * 256 + 2 * 1536
    B2_SZ = max(8 * D, 4 * D + ATT_TMP)
    M_SZ = max(4 * D, 16384)
    H1_O = 0
    B1_O = H1_O + H1_SZ
    B2_O = B1_O + B1_SZ
    M_O = B2_O + B2_SZ
    A_SZ = M_O + M_SZ
    ACT_CAP = (B2_SZ + M_SZ) // 1024

    stack = contextlib.ExitStack()
    with stack:
        A = stack.enter_context(nc.sbuf_tensor("A", [128, A_SZ // 4], F32))
        NRING = 3
        NSCRF = 6
        ring_t = stack.enter_context(nc.sbuf_tensor("ring", [128, NRING * 8 * 512], BF16))
        scrF_t = stack.enter_context(nc.sbuf_tensor("scrF", [128, NSCRF * 512], F32))
        scrB_t = stack.enter_context(nc.sbuf_tensor("scrB", [128, 4 * 512], BF16))
        ident = stack.enter_context(nc.sbuf_tensor("ident", [128, 128], BF16))
        ones = stack.enter_context(nc.sbuf_tensor("ones", [128, 128], BF16))
        vecs = stack.enter_context(nc.sbuf_tensor("vecs_sb", [128, NV], F32))
        qgs = stack.enter_context(nc.sbuf_tensor("qgs", [128, 1], F32))
        wsT = stack.enter_context(nc.sbuf_tensor("wsT_sb", [128, GW], BF16))
        bias2 = stack.enter_context(nc.sbuf_tensor("bias2", [128, GW], F32))
        small = stack.enter_context(nc.sbuf_tensor("small", [128, 64], F32))
        ps = stack.enter_context(nc.psum_tensor("ps", [128, 4096], F32))
        for t in (A, ring_t, scrF_t, scrB_t, ident, ones, vecs, qgs, wsT, bias2, small, ps):
            S.track(t.name)
        S.psum_names.add(ps.name)

        def aview(off, dt, shape):
            esz = mybir.dt.size(dt)
            n = 1
            for s_ in shape[1:]:
                n *= s_
            assert off % 4 == 0 and (n * esz) % 4 == 0
            ap = A[:, off // 4: off // 4 + n * esz // 4]
            if dt != F32:
                ap = ap.bitcast(dt)
            if len(shape) == 3:
                ap = ap.rearrange("p (a b) -> p a b", a=shape[1])
            elif len(shape) == 4:
                ap = ap.rearrange("p (a b c) -> p a b c", a=shape[1], b=shape[2])
            return ap

        h1 = aview(H1_O, F32, [128, 4, D])
        KnT = aview(H1_O, BF16, [128, NH, 1024])
        Vt = aview(H1_O + H1_SZ // 2, BF16, [128, 8, NW])
        gv = aview(H1_O, F32, [128, 4, GW])
        zt = aview(H1_O + H1_SZ // 2, BF16, [128, 4, GW])
        yAT = aview(H1_O + H1_SZ // 2 + 4 * D, BF16, [128, NG, T])
        xnT = aview(B1_O, BF16, [128, KC, T])
        xload = [aview(B2_O + i * 4 * D, F32, [128, D]) for i in range(2)]
        QnT = aview(B2_O, BF16, [128, NH, T])
        uT = aview(B2_O, BF16, [128, NG, T])
        ao = B2_O + 4 * D
        biasT = [aview(ao + i * 1792, BF16, [128, 7, 128]) for i in range(2)]
        nmask = aview(ao + 2 * 1792, BF16, [128, NM_TILES, 128])
        PT = [aview(ao + 2 * 1792 + NM_TILES * 256 + i * 1536, BF16, [128, 6, 128]) for i in range(2)]
        mergedT = aview(B2_O, BF16, [128, KC, T])
        actT = aview(B2_O, BF16, [128, ACT_CAP, T])
        xs = [aview(M_O + i * 2 * D, BF16, [128, D]) for i in range(2)]
        yBT = aview(M_O, BF16, [128, NH, T])
        xres = [aview(M_O + i * 8192, F32, [128, 4, 512]) for i in range(2)]

        ring3 = ring_t[:, :].rearrange("p (s k n) -> p s k n", s=NRING, k=8)
        ring = Ring([(i, ring3[:, i]) for i in range(NRING)])
        scrF = Ring([scrF_t[:, i * 512:(i + 1) * 512] for i in range(NSCRF)])
        scrF_raw = Ring([scrF_t[:, i * 512:(i + 1) * 512] for i in range(4)])
        scrF_rs = Ring([scrF_t[:, i * 512:(i + 1) * 512] for i in range(4, 6)])
        if H1_SZ // 2 >= 16384:
            mscr = [aview(H1_O + i * 2048, F32, [128, 512]) for i in range(8)]
        else:
            mscr_t = stack.enter_context(nc.sbuf_tensor("mscr", [128, 8 * 512], F32))
            S.track(mscr_t.name)
            mscr = [mscr_t[:, i * 512:(i + 1) * 512] for i in range(8)]
        scrB = Ring([scrB_t[:, i * 512:(i + 1) * 512] for i in range(4)])
        smallc = Ring([small[:, i:i + 1] for i in range(32)])
        sums4 = Ring([small[:, 32 + 4 * i: 36 + 4 * i] for i in range(8)])
        banks = Banks([ps[:, i * 512:(i + 1) * 512] for i in range(8)])
        xl_ring = Ring(list(enumerate(xload)))
        xs_ring = Ring(xs)
        xres_ring = Ring(list(enumerate(xres)))
        bt_ring = Ring(list(enumerate(biasT)))
        pt_ring = Ring(PT)
        flip = [0]

        g1T = vecs[:, 0:KC]
        g2T = vecs[:, KC:2 * KC]
        lngT = vecs[:, 2 * KC:2 * KC + NG]
        qg_raw = vecs[:, 2 * KC + NG:2 * KC + NG + 1]
        kg = vecs[:, 2 * KC + NG + 1:2 * KC + NG + 2]

        def act_op(out, in_, func, accum=None, scale=1.0, extra_reads=()):
            wr = [out] + ([accum] if accum is not None else [])
            rd = [in_] + list(extra_reads)
            if accum is not None:
                return S.op("act", lambda e: e.activation(out=out, in_=in_, func=func, scale=scale, accum_out=accum),
                            reads=rd, writes=wr)
            return S.op("act", lambda e: e.activation(out=out, in_=in_, func=func, scale=scale),
                        reads=rd, writes=wr)

        def ts_op(eng, out, in0, s1, s2, op0, op1=None):
            rd = [in0] + [s for s in (s1, s2) if not isinstance(s, (int, float, type(None)))]
            if op1 is None:
                return S.op(eng, lambda e: e.tensor_scalar(out=out, in0=in0, scalar1=s1, scalar2=None, op0=op0),
                            reads=rd, writes=[out])
            return S.op(eng, lambda e: e.tensor_scalar(out=out, in0=in0, scalar1=s1, scalar2=s2, op0=op0, op1=op1),
                        reads=rd, writes=[out])

        def tt_op(eng, out, in0, in1, op):
            return S.op(eng, lambda e: e.tensor_tensor(out=out, in0=in0, in1=in1, op=op),
                        reads=[in0, in1], writes=[out])

        def stt_op(eng, out, in0, scalar, in1, op0, op1):
            rd = [in0, in1] + ([scalar] if not isinstance(scalar, (int, float)) else [])
            return S.op(eng, lambda e: e.scalar_tensor_tensor(out=out, in0=in0, scalar=scalar, in1=in1, op0=op0, op1=op1),
                        reads=rd, writes=[out])

        def copy_op(out, in_):
            flip[0] ^= 1
            if flip[0]:
                return S.op("dve", lambda e: e.tensor_copy(out=out, in_=in_), reads=[in_], writes=[out])
            return act_op(out, in_, AF.Copy)

        def mm(out, lhsT, rhs, start, stop):
            return S.op("pe", lambda e: e.matmul(out, lhsT=lhsT, rhs=rhs, start=start, stop=stop),
                        reads=[lhsT, rhs], writes=[out])

        def rstd_op(out, in_, mult, eps):
            S.op("act", lambda e: e.activation(out=out, in_=in_, func=AF.Sqrt, scale=mult, bias=eps),
                 reads=[in_], writes=[out])
            S.op("dve", lambda e: e.reciprocal(out=out, in_=out), reads=[out], writes=[out])

        def load_piece(W, k0, k1, c0, w, pw=512):
            slot, rt = ring.next()
            if pw != 512:
                rt = rt.rearrange("p k n -> p (k n)").rearrange("p (k n) -> p k n", n=pw)
            dst = rt[:, 0:k1 - k0, 0:w]
            src = W[k0 * 128:k1 * 128, c0:c0 + w].rearrange("(kc p) n -> p kc n", p=128)
            S.dma("pool", dst, src, ("ring", slot))
            return rt

        def fm_job(W, Kc, c0, ncols, rhs_fn, evac, bw=512, ncol_fn=None):
            nblk = (ncols + bw - 1) // bw
            for b in range(nblk):
                w = min(bw, ncols - b * bw)
                nm = w // 128
                ba = [banks.alloc() for _ in range(nm)]
                bk = [x[1] for x in ba]
                kstep = 4096 // bw
                for k0 in range(0, Kc, kstep):
                    k1 = min(Kc, k0 + kstep)
                    rt = load_piece(W, k0, k1, c0 + b * bw, w, pw=bw)
                    for m in range(nm):
                        for kc in range(k0, k1):
                            n = 512 if ncol_fn is None else ncol_fn(kc)
                            mm(bk[m][:, 0:n], rt[:, kc - k0, m * 128:(m + 1) * 128], rhs_fn(kc)[:, 0:n],
                               kc == 0, kc == Kc - 1)
                older = S.take_deferred()
                evac(b, bk, nm)
                for x in ba:
                    banks.release(x[0])
                for fn in older:
                    fn()

        def tm_job(W, Kc, c0, ncols, lhsT_fn, evac, kbase=0):
            nblk = (ncols + 511) // 512
            for b in range(nblk):
                w = min(512, ncols - b * 512)
                ba = [banks.alloc() for _ in range(4)]
                bk = [x[1] for x in ba]
                for k0 in range(0, Kc, 8):
                    k1 = min(Kc, k0 + 8)
                    rt = load_piece(W, kbase + k0, kbase + k1, c0 + b * 512, w)
                    for kc in range(k0, k1):
                        for t in range(4):
                            mm(bk[t][:, 0:w], lhsT_fn(kc, t), rt[:, kc - k0, 0:w], kc == 0, kc == Kc - 1)
                older = S.take_deferred()
                evac(b, bk, w)
                for x in ba:
                    banks.release(x[0])
                for fn in older:
                    fn()

        S.dma("sp", vecs[:, :], vecs_d[:, :], "const", closed=True)
        S.dma("pool", wsT[:, :], wsT_d[:, :], "constp", closed=True)
        S.op("pool", lambda e: e.memset(ident[:, :], 0.0), writes=[ident[:, :]])
        S.op("pool", lambda e: e.affine_select(out=ident[:, :], in_=ident[:, :], pattern=[[-1, 128]],
                                               compare_op=ALU.not_equal, fill=1.0, base=0, channel_multiplier=1),
             reads=[ident[:, :]], writes=[ident[:, :]])
        S.op("pool", lambda e: e.memset(ones[:, :], 1.0), writes=[ones[:, :]])
        ts_op("dve", qgs[:, :], qg_raw, float(128 ** -0.5), None, ALU.mult)
        assert GW <= NSCRF * 512
        lnb_bc = scrF_t[:, 0:GW]
        lnb_bf = xs[0][:, 0:GW]
        bs_bc = aview(B2_O, F32, [128, GW])
        S.dma("sp", lnb_bc, bass.AP(lnb_d.tensor, 0, [[0, 128], [1, GW]]), "const", closed=True)
        S.dma("sp", bs_bc, bass.AP(bs_d.tensor, 0, [[0, 128], [1, GW]]), "const", closed=True)
        S.op("dve", lambda e: e.tensor_copy(out=lnb_bf, in_=lnb_bc), reads=[lnb_bc], writes=[lnb_bf])
        for g0 in range(0, NG, 4):
            n = min(4, NG - g0)
            bi_, bk = banks.alloc()
            for j in range(n):
                gi = g0 + j
                mm(bk[:, j * 128:(j + 1) * 128], lnb_bf[:, gi * 128:(gi + 1) * 128], wsT[:, gi * 128:(gi + 1) * 128], True, True)
            tt_op("dve", bias2[:, g0 * 128:(g0 + n) * 128], bk[:, 0:n * 128], bs_bc[:, g0 * 128:(g0 + n) * 128], ALU.add)
            banks.release(bi_)

        def norm_front(src):
            xsb = xs_ring.next()
            ssq = smallc.next()
            rstd = smallc.next()
            act_op(xsb, src, AF.Square, accum=ssq)
            rstd_op(rstd, ssq, 1.0 / D, RMS_EPS)
            ts_op("dve", xsb, src, rstd, None, ALU.mult)
            return xsb

        def norm_tile(src, gT, dst, pos):
            norm_back(norm_front(src), gT, dst, pos)

        def norm_back(xsb, gT, dst, pos):
            for kc0 in range(0, KC, 8):
                n = min(8, KC - kc0)
                bi_, bk = banks.alloc()
                bk = bk.bitcast(BF16)
                for j in range(n):
                    o = bk[:, j * 128:(j + 1) * 128]
                    i_ = xsb[:, (kc0 + j) * 128:(kc0 + j + 1) * 128]
                    S.op("pe", lambda e, o=o, i_=i_: e.transpose(out=o, in_=i_, identity=ident[:, :]),
                         reads=[i_, ident[:, :]], writes=[o])
                src3 = bk[:, 0:n * 128].rearrange("p (a b) -> p a b", a=n)
                tt_op("dve", dst[:, kc0:kc0 + n, pos * 128:(pos + 1) * 128], src3,
                      bc_last(gT[:, kc0:kc0 + n], 128), ALU.mult)
                banks.release(bi_)

        def load_x_tile(row0):
            slot, xl = xl_ring.next()
            S.dma("sp", xl, x_d[row0:row0 + 128, :], ("xl", slot))
            return xl

        def evac_qk(bk, nm, h0, dests, gain):
            for m in range(nm):
                bank = bk[m]
                sq = scrB.next()
                raw = scrF_raw.next()
                act_op(raw, bank, AF.Copy)
                tt_op("dve", sq, raw, raw, ALU.mult)

                def pe_work(sq=sq, raw=raw, h=h0 + m):
                    bi_, ssb = banks.alloc()
                    mm(ssb, ones[:, :], sq, True, True)
                    rs = scrF_rs.next()
                    rstd_op(rs, ssb, 1.0 / 128, RMS_EPS)
                    banks.release(bi_)
                    for dst_fn, lo, hi in dests:
                        stt_op("dve", dst_fn(h), raw[:, lo:hi], gain, rs[:, lo:hi], ALU.mult, ALU.mult)
                S.defer(pe_work)

        def group(g):
            S.epoch = g + 1

            stored = [False]

            def stop_at(k):
                return getattr(cfg, "STOP", 99) <= k

            def finish():
                S.flush()
                if stored[0]:
                    return
                def stores(g=g):
                    for t in range(4):
                        r0 = g * T + t * 128
                        S.dma("act", out_d[r0:r0 + 128, :], h1[:, t, :], ("out", t))
                pending_out.append(stores)
            if stop_at(0):
                return finish()
            base = g * T
            xn_rhs = lambda kc: xnT[:, kc, :]
            xn_lhsT = lambda kc, t: xnT[:, kc, t * 128:(t + 1) * 128]

            hs_ = (0, 1, 6, 7)
            xls = [load_x_tile(base + hs_[0] * 128), load_x_tile(base + hs_[1] * 128)]
            while pending_out:
                pending_out.pop(0)()
            for pos in range(4):
                norm_tile(xls[pos], g1T, xnT, pos)
                if pos + 2 < 4:
                    xls.append(load_x_tile(base + hs_[pos + 2] * 128))
            if stop_at(1):
                return finish()
            own_front = {}
            for pos in range(2):
                own_front[pos] = norm_front(load_x_tile(base + (2 + pos) * 128))
            halo_dests = [(lambda h: KnT[:, h, 0:256], 0, 256), (lambda h: KnT[:, h, 768:1024], 256, 512)]
            fm_job(w_in, KC, cfg.K0, NW, xn_rhs,
                   lambda b, bk, nm: evac_qk(bk, nm, b * 2, halo_dests, kg), bw=256,
                   ncol_fn=lambda kc: 512 if kc == 0 else 448)
            hslots = (0, 1, 6, 7)

            def evac_v_halo(b, bk, w):
                for t in range(4):
                    copy_op(Vt[:, hslots[t], b * 512:b * 512 + w], bk[t][:, 0:w])
            tm_job(w_in, KC, cfg.VV0, NW, xn_lhsT, evac_v_halo)
            S.flush()

            if stop_at(2):
                return finish()
            for pos in range(4):
                norm_back(own_front[pos], g1T, xnT, pos)
                if pos + 2 < 4:
                    own_front[pos + 2] = norm_front(load_x_tile(base + (2 + pos + 2) * 128))
            own_dests = [(lambda h: KnT[:, h, 256:768], 0, 512)]
            fm_job(w_in, KC, cfg.K0, NW, xn_rhs,
                   lambda b, bk, nm: evac_qk(bk, nm, b * 2, own_dests, kg), bw=256)

            S.dma("pool", nmask.rearrange("p a b -> p (a b)"), nm_d[g * 128:(g + 1) * 128, :], "nmask")
            bt_pre = bt_ring.next()
            S.dma("pool", bt_pre[1].rearrange("p a b -> p (a b)"), bias_d[0:128, :], ("bt", bt_pre[0]))
            def evac_v_own(b, bk, w):
                for t in range(4):
                    copy_op(Vt[:, 2 + t, b * 512:b * 512 + w], bk[t][:, 0:w])
            tm_job(w_in, KC, cfg.VV0, NW, xn_lhsT, evac_v_own)
            q_dests = [(lambda h: QnT[:, h, :], 0, 512)]
            fm_job(w_in, KC, cfg.Q0, NW, xn_rhs,
                   lambda b, bk, nm: evac_qk(bk, nm, b * 2, q_dests, qgs[:, :]), bw=256)
            if NH <= 2:
                S.flush()

            if stop_at(3):
                return finish()
            pend = []

            def pv(h, s, pt, yba):
                yb = [x[1] for x in yba]
                qi = s - 2
                lo, hi = KB[s]
                nb = hi - lo
                col = (qi % 2) * 256
                ybank = yb[qi // 2]
                for jj in range(nb):
                    mm(ybank[:, col:col + 128], Vt[:, lo + jj, h * 128:(h + 1) * 128], pt[:, jj, :], jj == 0, jj == nb - 1)
                for jj in range(nb):
                    mm(ybank[:, col + 128:col + 256], ones[:, :], pt[:, jj, :], jj == 0, jj == nb - 1)

            def fin_head(h, yba):
                yb = [x[1] for x in yba]
                for x in yba:
                    banks.release(x[0])
                for half in range(2):
                    v3 = yb[half].rearrange("p (a b) -> p a b", a=2)
                    rd = scrF.next()[:, 0:256].rearrange("p (a b) -> p a b", a=2)
                    S.op("dve", lambda e, rd=rd, v3=v3: e.reciprocal(out=rd, in_=v3[:, :, 128:256]),
                         reads=[v3[:, :, 128:256]], writes=[rd])
                    dst = yBT[:, h, half * 256:(half + 1) * 256].rearrange("p (a b) -> p a b", a=2)
                    tt_op("dve", dst, v3[:, :, 0:128], rd, ALU.mult)

            for h in range(NH):
                if h == 0:
                    bslot, bt = bt_pre
                else:
                    bslot, bt = bt_ring.next()
                    S.dma("pool", bt.rearrange("p a b -> p (a b)"), bias_d[h * 128:(h + 1) * 128, :], ("bt", bslot))
                yb = [banks.alloc() for _ in range(2)]
                for s in (2, 3, 4, 5):
                    qi = s - 2
                    lo, hi = KB[s]
                    nb = hi - lo
                    sba = [banks.alloc() for _ in range(2)]
                    sb = [x[1] for x in sba]
                    for jj in range(nb):
                        j = lo + jj
                        o = sb[jj // 4][:, (jj % 4) * 128:(jj % 4 + 1) * 128]
                        mm(o, KnT[:, h, j * 128:(j + 1) * 128], QnT[:, h, qi * 128:(qi + 1) * 128], True, False)
                        mm(o, ident[:, :], bt[:, j - s + 3, :], False, False)
                        mm(o, ident[:, :], nmask[:, NM_OFF[s] + jj, :], False, True)
                    pt = pt_ring.next()
                    n0 = min(4, nb)
                    act_op(pt[:, 0:n0, :], sb[0][:, 0:n0 * 128].rearrange("p (a b) -> p a b", a=n0), AF.Exp)
                    if nb > 4:
                        act_op(pt[:, 4:nb, :], sb[1][:, 0:(nb - 4) * 128].rearrange("p (a b) -> p a b", a=nb - 4), AF.Exp)
                    for x in sba:
                        banks.release(x[0])
                    if pend:
                        a_ = pend.pop()
                        pv(*a_[:4])
                        if a_[4]:
                            fin_head(a_[0], a_[3])
                    pend.append((h, s, pt, yb, s == 5))
                if h == 0:
                    S.flush()
            a_ = pend.pop()
            pv(*a_[:4])
            fin_head(a_[0], a_[3])

            if stop_at(4):
                return finish()
            nvb = (GW + 511) // 512
            sums = [sums4.next() for _ in range(4)]

            def evac_va(b, bk, w):
                for t in range(4):
                    act_op(gv[:, t, b * 512:b * 512 + w], bk[t][:, 0:w], AF.Gelu, accum=sums[t][:, b:b + 1])
            tm_job(w_in, KC, cfg.V0, GW, xn_lhsT, evac_va)
            S.flush()
            for t in range(4):
                ssq = smallc.next()
                mean = smallc.next()
                var = smallc.next()
                tot = smallc.next()
                act_op(zt[:, t, :], gv[:, t, :], AF.Square, accum=ssq)
                S.op("dve", lambda e, tot=tot, s_=sums[t]: e.reduce_sum(out=tot, in_=s_[:, 0:nvb], axis=mybir.AxisListType.X),
                     reads=[sums[t][:, 0:nvb]], writes=[tot])
                ts_op("dve", mean, tot, 1.0 / GW, None, ALU.mult)
                tt_op("dve", var, mean, mean, ALU.mult)
                stt_op("dve", var, ssq, 1.0 / GW, var, ALU.mult, ALU.subtract)
                rstd_op(var, var, 1.0, LN_EPS)
                ts_op("dve", zt[:, t, :], gv[:, t, :], mean, var, ALU.subtract, ALU.mult)
            def evac_u(b, bk, nm):
                for m in range(nm):
                    act_op(uT[:, b * 4 + m, :], bk[m], AF.Gelu)
            fm_job(w_in, KC, cfg.U0, GW, xn_rhs, evac_u)
            for t in range(4):
                for g0 in range(0, NG, 4):
                    n = min(4, NG - g0)
                    bi_, bk = banks.alloc()
                    for j in range(n):
                        gi = g0 + j
                        mm(bk[:, j * 128:(j + 1) * 128], zt[:, t, gi * 128:(gi + 1) * 128], wsT[:, gi * 128:(gi + 1) * 128], True, True)
                    tmp = scrF.next()
                    for j in range(n):
                        gi = g0 + j
                        stt_op("dve", tmp[:, j * 128:(j + 1) * 128], bk[:, j * 128:(j + 1) * 128], lngT[:, gi:gi + 1],
                               bias2[:, gi * 128:(gi + 1) * 128], ALU.mult, ALU.add)
                    tt_op("dve", yAT[:, g0:g0 + n, t * 128:(t + 1) * 128],
                          tmp[:, 0:n * 128].rearrange("p (a b) -> p a b", a=n),
                          uT[:, g0:g0 + n, t * 128:(t + 1) * 128], ALU.mult)
                    banks.release(bi_)

            if stop_at(5):
                return finish()
            ya_rhs = lambda kc: yAT[:, kc, :]
            yb_rhs = lambda kc: yBT[:, kc, :]
            for cb in range(D // 512):
                sa = mscr[0:4]

                def evac_ga(b, bk, nm, sa=sa):
                    for m in range(nm):
                        act_op(sa[m], bk[m], AF.Sigmoid)
                fm_job(w_in, KC, cfg.GA0 + cb * 512, 512, xn_rhs, evac_ga)

                def evac_a(b, bk, nm, sa=sa):
                    for m in range(nm):
                        tt_op("dve", sa[m], bk[m], sa[m], ALU.mult)
                fm_job(w_oa, NG, cb * 512, 512, ya_rhs, evac_a)
                sbb = mscr[4:8]

                def evac_gb(b, bk, nm, sbb=sbb):
                    for m in range(nm):
                        act_op(sbb[m], bk[m], AF.Sigmoid)
                fm_job(w_in, KC, cfg.GB0 + cb * 512, 512, xn_rhs, evac_gb)

                def evac_b(b, bk, nm, sa=sa, sbb=sbb, cb=cb):
                    for m in range(nm):
                        tt_op("dve", sbb[m], bk[m], sbb[m], ALU.mult)
                        tt_op("dve", mergedT[:, cb * 4 + m, :], sa[m], sbb[m], ALU.add)
                fm_job(w_ob, NH, cb * 512, 512, yb_rhs, evac_b)

            if stop_at(6):
                return finish()
            own_row0 = base + 256

            def evac_out(b, bk, w):
                slot, xr = xres_ring.next()
                src = x_d[own_row0:own_row0 + 512, b * 512:b * 512 + w].rearrange("(t p) n -> p t n", p=128)
                S.dma("sp", xr[:, :, 0:w], src, ("xres", slot))
                for t in range(4):
                    tt_op("dve", h1[:, t, b * 512:b * 512 + w], bk[t][:, 0:w], xr[:, t, 0:w], ALU.add)
            tm_job(w_out, KC, 0, D, lambda kc, t: mergedT[:, kc, t * 128:(t + 1) * 128], evac_out)

            if stop_at(7):
                return finish()
            for t in range(4):
                norm_tile(h1[:, t, :], g2T, xnT, t)

            if stop_at(8):
                return finish()
            FC = cfg.FC
            cap = (min(ACT_CAP, getattr(cfg, 'ACT_CAP_OVERRIDE', ACT_CAP)) // 4) * 4
            nparts = (FC + cap - 1) // cap
            per = ((FC + nparts - 1) // nparts + 3) // 4 * 4
            c_lo = 0
            while c_lo < FC:
                c_hi = min(FC, c_lo + per)
                nch = c_hi - c_lo
                for cb0 in range(c_lo, c_hi, 4):
                    n4 = min(4, c_hi - cb0)
                    sg = [scrF.next() for _ in range(n4)]

                    def evac_gate(b, bk, nm, sg=sg):
                        for m in range(nm):
                            act_op(sg[m], bk[m], AF.Silu)
                    fm_job(w_g, KC, cb0 * 128, n4 * 128, xn_rhs, evac_gate)

                    def evac_up(b, bk, nm, sg=sg, cb0=cb0, c_lo=c_lo):
                        for m in range(nm):
                            tt_op("dve", actT[:, cb0 - c_lo + m, :], bk[m], sg[m], ALU.mult)
                    fm_job(w_u, KC, cb0 * 128, n4 * 128, xn_rhs, evac_up)

                last_part = c_hi == FC

                def evac_down(b, bk, w, last_part=last_part):
                    for t in range(4):
                        tt_op("dve", h1[:, t, b * 512:b * 512 + w], bk[t][:, 0:w], h1[:, t, b * 512:b * 512 + w], ALU.add)
                    if last_part:
                        dst = out_d[g * T:g * T + 512, b * 512:b * 512 + w].rearrange("(t p) n -> p t n", p=128)
                        S.dma("act", dst, h1[:, :, b * 512:b * 512 + w], ("out", b))
                        stored[0] = True
                tm_job(w_d, nch, 0, D, lambda kc, t: actT[:, kc, t * 128:(t + 1) * 128], evac_down, kbase=c_lo)
                c_lo = c_hi

            return finish()

        pending_out = []
        for g in range(cfg.NGRP):
            group(g)
        while pending_out:
            pending_out.pop(0)()
        S.out_dmas = [k for k in S.dma_cum if isinstance(k, tuple) and k[0] == "out"]

        S.prepare(nc, stack)
        with nc.Block() as block:
            S.emit(block)
    return nc, S


def _bias_table(rpb):
    NH = rpb.shape[0]
    k = np.arange(128)
    q = np.arange(128)
    krl, kc = k // 64, k % 64
    qrl, qc = q // 64, q % 64
    tab = np.zeros((NH, 128, 7, 128), np.float32)
    for ri, rho in enumerate(range(-3, 4)):
        dr = 2 * rho + krl[:, None] - qrl[None, :]
        dc = kc[:, None] - qc[None, :]
        ok = (np.abs(dr) <= 7) & (np.abs(dc) <= 15)
        ir = np.clip(dr + 7, 0, 14)
        ic = np.clip(dc + 15, 0, 30)
        vals = rpb[:, ir, ic]
        tab[:, :, ri, :] = np.where(ok[None], vals, np.float32(0.0))
    return np.ascontiguousarray(tab.reshape(NH * 128, 7 * 128))


def _neg_mask(cfg, core):
    ROWS = cfg.ROWS
    out = np.full((cfg.NGRP, 128, NM_TILES, 128), NEG, np.float32)
    k = np.arange(128)
    q = np.arange(128)
    krl, kcc = k // 64, k % 64
    qrl, qcc = q // 64, q % 64
    cs = np.clip(qcc - 8, 0, GRID_W - 16)
    for g in range(cfg.NGRP):
        for s in (2, 3, 4, 5):
            tq = core * cfg.TPC + 4 * g + (s - 2)
            r = 2 * tq + qrl
            rs = np.clip(r - 4, 0, ROWS - 8)
            cnt = np.zeros(128, np.int64)
            lo, hi = KB[s]
            for jj, j in enumerate(range(lo, hi)):
                tk = core * cfg.TPC + 4 * g + (j - 2)
                kr = 2 * tk + krl
                ok = (kr[:, None] >= 0) & (kr[:, None] < ROWS) \
                    & (kr[:, None] >= rs[None, :]) & (kr[:, None] < rs[None, :] + 8) \
                    & (kcc[:, None] >= cs[None, :]) & (kcc[:, None] < cs[None, :] + 16)
                out[g, :, NM_OFF[s] + jj, :] = np.where(ok, np.float32(0.0), np.float32(NEG))
                cnt += ok.sum(0)
            assert (cnt == 128).all(), (core, g, s, cnt)
    return np.ascontiguousarray(out.reshape(cfg.NGRP * 128, NM_TILES * 128))


_CACHE = {}


def run(cfg, x, norm1_g, w_in, gm_ln_g, gm_ln_b, gm_w_s, gm_b_s, q_gain, k_gain, na_rpb,
        w_o_gm, w_o_na, w_out, norm2_g, w_ff_gate, w_ff_up, w_ff_down, trace=False):
    f = lambda a: np.ascontiguousarray(np.asarray(a, np.float32))
    D, KC, NG = cfg.D, cfg.KC, cfg.NG
    x2 = f(x).reshape(cfg.SEQ, D)
    xpad = np.zeros((cfg.SEQ + 512, D), np.float32)
    xpad[256:256 + cfg.SEQ] = x2
    vecs = np.concatenate([
        f(norm1_g).reshape(KC, 128).T, f(norm2_g).reshape(KC, 128).T,
        f(gm_ln_g).reshape(NG, 128).T, f(q_gain).reshape(1, 128).T, f(k_gain).reshape(1, 128).T], axis=1)
    shared = {
        "w_in": f(w_in).reshape(D, cfg.IN_COLS), "w_o_gm": f(w_o_gm).reshape(cfg.GW, D),
        "w_o_na": f(w_o_na).reshape(cfg.NW, D), "w_out": f(w_out).reshape(D, D),
        "w_ff_gate": f(w_ff_gate).reshape(D, cfg.DFF), "w_ff_up": f(w_ff_up).reshape(D, cfg.DFF),
        "w_ff_down": f(w_ff_down).reshape(cfg.DFF, D),
        "vecs": np.ascontiguousarray(vecs),
        "lnb_row": f(gm_ln_b).reshape(1, cfg.GW), "bs_row": f(gm_b_s).reshape(1, cfg.GW),
        "wsT": np.ascontiguousarray(f(gm_w_s).reshape(NG, 128, 128).transpose(2, 0, 1).reshape(128, cfg.GW)),
        "biasT": _bias_table(f(na_rpb).reshape(cfg.NH, 15, 31)),
    }
    if getattr(cfg, "TINYW", False):
        for k in ("w_in", "w_o_gm", "w_o_na", "w_out", "w_ff_gate", "w_ff_up", "w_ff_down"):
            shared[k] = np.zeros((128, 128), np.float32)
    in_maps = []
    for c in range(cfg.NCORES):
        m = dict(shared)
        m["x"] = xpad[c * cfg.TOK: c * cfg.TOK + cfg.TOK + 512]
        m["negmask"] = _neg_mask(cfg, c)
        in_maps.append(m)
    key = (cfg.D, cfg.SEQ, cfg.DFF)
    if key not in _CACHE:
        _CACHE[key] = build(cfg)[0]
    nc = _CACHE[key]
    res = run_bass_kernel_spmd(nc, in_maps, core_ids=list(range(cfg.NCORES)), trace=trace)
    out = np.concatenate([res.results[c]["out"] for c in range(cfg.NCORES)], axis=0)
    return out.reshape(1, cfg.SEQ, D).astype(np.float32, copy=False), res


def kernel(**inputs):
    cfg = Cfg()
    out, _ = run(cfg, **inputs)
    return out
```

```python
import contextlib
import numpy as np
import concourse.bass as bass
import concourse.mybir as mybir
from concourse.bass_utils import run_bass_kernel_spmd

F32 = mybir.dt.float32
BF16 = mybir.dt.bfloat16
AF = mybir.ActivationFunctionType
ALU = mybir.AluOpType

GRID_W = 64
RMS_EPS = 1e-6
LN_EPS = 1e-5
NEG = -30000.0
T = 512
KB = {2: (0, 6), 3: (1, 6), 4: (2, 7), 5: (2, 8)}
NM_OFF = {2: 0, 3: 6, 4: 11, 5: 16}
NM_TILES = 22
ENGS = ("pe", "act", "dve", "pool", "sp")


class Cfg:
    def __init__(self, D=4096, SEQ=16384, DFF=11008, NCORES=8):
        self.D, self.SEQ, self.DFF, self.NCORES = D, SEQ, DFF, NCORES
        self.GW = D // 2
        self.NW = D // 2
        self.KC = D // 128
        self.NG = self.GW // 128
        self.NH = self.NW // 128
        self.FC = DFF // 128
        self.IN_COLS = 2 * self.GW + 3 * self.NW + 2 * D
        self.TOK = SEQ // NCORES
        self.NGRP = self.TOK // T
        self.TPC = self.TOK // 128
        self.ROWS = SEQ // GRID_W
        self.U0 = 0
        self.V0 = self.GW
        self.Q0 = 2 * self.GW
        self.K0 = self.Q0 + self.NW
        self.VV0 = self.K0 + self.NW
        self.GA0 = self.VV0 + self.NW
        self.GB0 = self.GA0 + D
        assert self.TOK % T == 0 and D % 256 == 0 and DFF % 128 == 0


class Instr:
    __slots__ = ("eng", "fn", "deps", "same", "is_dma", "dkey", "dcum", "signal", "rank", "epoch", "idx")


class Sched:
    BUCKET = 16384

    def __init__(self):
        self.streams = {e: [] for e in ENGS}
        self.acc = {}
        self.dma_cum = {}
        self.closed = set()
        self.epoch = 0
        self.n = 0
        self.tracked = set()
        self.psum_names = set()
        self.deferred = []
        self.out_dmas = []

    def track(self, name):
        self.tracked.add(name)

    def _rng(self, ap):
        name = ap.tensor.name
        if name not in self.tracked:
            return None
        pstep = ap.ap[0][0]
        esz = mybir.dt.size(ap.dtype)
        off = int(ap.offset) % pstep
        ext = 1
        for st, cnt in ap.ap[1:]:
            ext += st * (cnt - 1)
        return name, off * esz, (off + ext) * esz

    def _buckets(self, name, lo, hi):
        b0 = lo // self.BUCKET
        b1 = (hi - 1) // self.BUCKET
        for b in range(b0, b1 + 1):
            yield (name, b), max(lo, b * self.BUCKET), min(hi, (b + 1) * self.BUCKET)

    def op(self, eng, fn, reads=(), writes=(), dkey=None, closed=False):
        ins = Instr()
        ins.eng = eng
        ins.fn = fn
        ins.is_dma = dkey is not None
        ins.dkey = dkey
        ins.signal = False
        ins.rank = 0
        ins.epoch = self.epoch
        ins.idx = self.n
        self.n += 1
        if ins.is_dma:
            c = self.dma_cum.get(dkey, 0) + 16
            self.dma_cum[dkey] = c
            ins.dcum = c
            if closed:
                self.closed.add(dkey)
        deps = {}
        dmadeps = {}
        same = None
        accs = []
        for a, isw in [(x, False) for x in reads] + [(x, True) for x in writes]:
            r = self._rng(a)
            if r is None:
                continue
            name, lo, hi = r
            if name in self.psum_names:
                lo = (lo // 2048) * 2048
                hi = ((hi + 2047) // 2048) * 2048
                accs.append((name, lo, hi, True, isw))
            else:
                accs.append((name, lo, hi, isw, isw))

        def add_dep(p, kind):
            nonlocal same
            if p is ins:
                return
            if p.is_dma:
                if dmadeps.get(p.dkey, 0) < p.dcum:
                    dmadeps[p.dkey] = p.dcum
            elif p.eng == eng and not ins.is_dma:
                if eng != "pe" and kind != "war":
                    if same is None or same.idx < p.idx:
                        same = p
            elif p.eng == eng and ins.is_dma:
                if same is None or same.idx < p.idx:
                    same = p
            else:
                q = deps.get(p.eng)
                if q is None or q.idx < p.idx:
                    deps[p.eng] = p

        for name, lo0, hi0, cw, rw in accs:
            for key, lo, hi in self._buckets(name, lo0, hi0):
                for e in self.acc.get(key, ()):
                    if e[0] < hi and lo < e[1] and (cw or e[3]):
                        if e[4]:
                            add_dep(e[2], "waw" if rw else "raw")
                        else:
                            add_dep(e[2], "war")
        for name, lo0, hi0, cw, rw in accs:
            for key, lo, hi in self._buckets(name, lo0, hi0):
                lst = self.acc.get(key)
                if lst is None:
                    lst = self.acc[key] = []
                if cw:
                    lst[:] = [e for e in lst if e[2] is ins or not (lo <= e[0] and e[1] <= hi)]
                    lst.append([lo, hi, ins, True, rw])
                else:
                    for e in lst:
                        if (not e[3]) and e[0] == lo and e[1] == hi and e[2].eng == eng \
                                and (not e[2].is_dma) and (not ins.is_dma):
                            e[2] = ins
                            break
                    else:
                        lst.append([lo, hi, ins, False, False])
        ins.deps = (list(deps.values()), dmadeps)
        ins.same = same
        self.streams[eng].append(ins)
        return ins

    def dma(self, queue, out, in_, key, closed=False):
        return self.op(queue, lambda e: e.dma_start(out=out, in_=in_),
                       reads=[in_], writes=[out], dkey=key, closed=closed)

    def defer(self, fn):
        self.deferred.append(fn)

    def take_deferred(self):
        d = self.deferred
        self.deferred = []
        return d

    def flush(self):
        d = self.deferred
        self.deferred = []
        for fn in d:
            fn()

    def prepare(self, nc, stack):
        for s in self.streams.values():
            for ins in s:
                for p in ins.deps[0]:
                    p.signal = True
                if ins.same is not None:
                    ins.same.signal = True
        engsem = {}
        for eng, s in self.streams.items():
            cnt = {}
            for ins in s:
                if ins.signal:
                    k = (eng, ins.epoch)
                    cnt[k] = cnt.get(k, 0) + 1
                    ins.rank = cnt[k]
                    if k not in engsem:
                        engsem[k] = stack.enter_context(nc.semaphore("s_%s_%d" % k))
        dmasem = {}
        for i, k in enumerate(self.dma_cum):
            dmasem[k] = stack.enter_context(nc.semaphore("d_%d" % i))
        self.n_sems = len(engsem) + len(dmasem)
        self.engsem, self.dmasem = engsem, dmasem

    def emit(self, block):
        engsem, dmasem = self.engsem, self.dmasem

        def run(eng, e):
            waited = {}

            def w(sem, val):
                if waited.get(sem, 0) >= val:
                    return
                waited[sem] = val
                e.wait_ge(sem, val)

            for ins in self.streams[eng]:
                prods, dmadeps = ins.deps
                for p in prods:
                    w(engsem[(p.eng, p.epoch)], p.rank)
                if ins.same is not None:
                    p = ins.same
                    w(engsem[(p.eng, p.epoch)], p.rank)
                for k, c in dmadeps.items():
                    if k in self.closed:
                        c = self.dma_cum[k]
                    w(dmasem[k], c)
                bi = ins.fn(e)
                if ins.is_dma:
                    bi.then_inc(dmasem[ins.dkey], 16)
                elif ins.signal:
                    bi.then_inc(engsem[(eng, ins.epoch)], 1)
            if eng == "sp":
                for k in self.out_dmas:
                    w(dmasem[k], self.dma_cum[k])

        @block.tensor
        def _(e):
            run("pe", e)

        @block.scalar
        def _(e):
            run("act", e)

        @block.vector
        def _(e):
            run("dve", e)

        @block.gpsimd
        def _(e):
            run("pool", e)

        @block.sync
        def _(e):
            run("sp", e)


class Banks:
    def __init__(self, aps):
        self.aps = aps
        self.busy = [False] * len(aps)
        self.stamp = list(range(len(aps)))
        self.clock = len(aps)

    def alloc(self):
        free = [j for j in range(len(self.aps)) if not self.busy[j]]
        if not free:
            raise RuntimeError("all PSUM banks busy")
        j = min(free, key=lambda k: self.stamp[k])
        self.busy[j] = True
        return j, self.aps[j]

    def release(self, j):
        assert self.busy[j]
        self.busy[j] = False
        self.stamp[j] = self.clock
        self.clock += 1


class Ring:
    def __init__(self, items):
        self.items = items
        self.i = 0

    def next(self):
        it = self.items[self.i % len(self.items)]
        self.i += 1
        return it


def bc_last(ap, n):
    a = [list(x) for x in ap.ap]
    return bass.AP(ap.tensor, ap.offset, a + [[0, n]])


def build(cfg):
    D, KC, NG, NH, GW, NW, DFF = cfg.D, cfg.KC, cfg.NG, cfg.NH, cfg.GW, cfg.NW, cfg.DFF
    nc = bass.Bass("TRN2", target_bir_lowering=False)
    S = Sched()

    def dram_in(name, shape):
        return nc.dram_tensor(name, list(shape), F32, kind="ExternalInput").ap()

    x_d = dram_in("x", [cfg.TOK + 512, D])
    tiny = getattr(cfg, "TINYW", False)
    wshape = lambda sh: [128, 128] if tiny else sh
    w_in = dram_in("w_in", wshape([D, cfg.IN_COLS]))
    w_oa = dram_in("w_o_gm", wshape([GW, D]))
    w_ob = dram_in("w_o_na", wshape([NW, D]))
    w_out = dram_in("w_out", wshape([D, D]))
    w_g = dram_in("w_ff_gate", wshape([D, DFF]))
    w_u = dram_in("w_ff_up", wshape([D, DFF]))
    w_d = dram_in("w_ff_down", wshape([DFF, D]))
    NV = 2 * KC + NG + 2
    vecs_d = dram_in("vecs", [128, NV])
    lnb_d = dram_in("lnb_row", [1, GW])
    bs_d = dram_in("bs_row", [1, GW])
    wsT_d = dram_in("wsT", [128, GW])
    bias_d = dram_in("biasT", [NH * 128, 7 * 128])
    nm_d = dram_in("negmask", [cfg.NGRP * 128, NM_TILES * 128])
    out_d = nc.dram_tensor("out", [cfg.TOK, D], F32, kind="ExternalOutput").ap()

    H1_SZ = 16 * D
    B1_SZ = 8 * D
    ATT_TMP = 2 * 1792 + NM_TILES * 256 + 2 * 1536
    B2_SZ = max(8 * D, 4 * D + ATT_TMP)
    M_SZ = max(4 * D, 16384)
    H1_O = 0
    B1_O = H1_O + H1_SZ
    B2_O = B1_O + B1_SZ
    M_O = B2_O + B2_SZ
    A_SZ = M_O + M_SZ
    ACT_CAP = (B2_SZ + M_SZ) // 1024

    stack = contextlib.ExitStack()
    with stack:
        A = stack.enter_context(nc.sbuf_tensor("A", [128, A_SZ // 4], F32))
        NRING = 4
        NSCRF = 6
        ring_t = stack.enter_context(nc.sbuf_tensor("ring", [128, NRING * 8 * 512], BF16))
        scrF_t = stack.enter_context(nc.sbuf_tensor("scrF", [128, NSCRF * 512], F32))
        scrB_t = stack.enter_context(nc.sbuf_tensor("scrB", [128, 4 * 512], BF16))
        ident = stack.enter_context(nc.sbuf_tensor("ident", [128, 128], BF16))
        ones = stack.enter_context(nc.sbuf_tensor("ones", [128, 128], BF16))
        vecs = stack.enter_context(nc.sbuf_tensor("vecs_sb", [128, NV], F32))
        qgs = stack.enter_context(nc.sbuf_tensor("qgs", [128, 1], F32))
        wsT = stack.enter_context(nc.sbuf_tensor("wsT_sb", [128, GW], BF16))
        bias2 = stack.enter_context(nc.sbuf_tensor("bias2", [128, GW], F32))
        small = stack.enter_context(nc.sbuf_tensor("small", [128, 64], F32))
        ps = stack.enter_context(nc.psum_tensor("ps", [128, 4096], F32))
        for t in (A, ring_t, scrF_t, scrB_t, ident, ones, vecs, qgs, wsT, bias2, small, ps):
            S.track(t.name)
        S.psum_names.add(ps.name)

        def aview(off, dt, shape):
            esz = mybir.dt.size(dt)
            n = 1
            for s_ in shape[1:]:
                n *= s_
            assert off % 4 == 0 and (n * esz) % 4 == 0
            ap = A[:, off // 4: off // 4 + n * esz // 4]
            if dt != F32:
                ap = ap.bitcast(dt)
            if len(shape) == 3:
                ap = ap.rearrange("p (a b) -> p a b", a=shape[1])
            elif len(shape) == 4:
                ap = ap.rearrange("p (a b c) -> p a b c", a=shape[1], b=shape[2])
            return ap

        h1 = aview(H1_O, F32, [128, 4, D])
        KnT = aview(H1_O, BF16, [128, NH, 1024])
        Vt = aview(H1_O + H1_SZ // 2, BF16, [128, 8, NW])
        gv = aview(H1_O, F32, [128, 4, GW])
        zt = aview(H1_O + H1_SZ // 2, BF16, [128, 4, GW])
        yAT = aview(H1_O + H1_SZ // 2 + 4 * D, BF16, [128, NG, T])
        xnT = aview(B1_O, BF16, [128, KC, T])
        xload = [aview(B2_O + i * 4 * D, F32, [128, D]) for i in range(2)]
        QnT = aview(B2_O, BF16, [128, NH, T])
        uT = aview(B2_O, BF16, [128, NG, T])
        ao = B2_O + 4 * D
        biasT = [aview(ao + i * 1792, BF16, [128, 7, 128]) for i in range(2)]
        nmask = aview(ao + 2 * 1792, BF16, [128, NM_TILES, 128])
        PT = [aview(ao + 2 * 1792 + NM_TILES * 256 + i * 1536, BF16, [128, 6, 128]) for i in range(2)]
        mergedT = aview(B2_O, BF16, [128, KC, T])
        actT = aview(B2_O, BF16, [128, ACT_CAP, T])
        xs = [aview(M_O + i * 2 * D, BF16, [128, D]) for i in range(2)]
        yBT = aview(M_O, BF16, [128, NH, T])
        xres = [aview(M_O + i * 8192, F32, [128, 4, 512]) for i in range(2)]

        ring3 = ring_t[:, :].rearrange("p (s k n) -> p s k n", s=NRING, k=8)
        ring = Ring([(i, ring3[:, i]) for i in range(NRING)])
        scrF = Ring([scrF_t[:, i * 512:(i + 1) * 512] for i in range(NSCRF)])
        scrF_raw = Ring([scrF_t[:, i * 512:(i + 1) * 512] for i in range(4)])
        scrF_rs = Ring([scrF_t[:, i * 512:(i + 1) * 512] for i in range(4, 6)])
        if H1_SZ // 2 >= 16384:
            mscr = [aview(H1_O + i * 2048, F32, [128, 512]) for i in range(8)]
        else:
            mscr_t = stack.enter_context(nc.sbuf_tensor("mscr", [128, 8 * 512], F32))
            S.track(mscr_t.name)
            mscr = [mscr_t[:, i * 512:(i + 1) * 512] for i in range(8)]
        scrB = Ring([scrB_t[:, i * 512:(i + 1) * 512] for i in range(4)])
        smallc = Ring([small[:, i:i + 1] for i in range(32)])
        sums4 = Ring([small[:, 32 + 4 * i: 36 + 4 * i] for i in range(8)])
        banks = Banks([ps[:, i * 512:(i + 1) * 512] for i in range(8)])
        xl_ring = Ring(list(enumerate(xload)))
        xs_ring = Ring(xs)
        xres_ring = Ring(list(enumerate(xres)))
        bt_ring = Ring(list(enumerate(biasT)))
        pt_ring = Ring(PT)
        flip = [0]

        g1T = vecs[:, 0:KC]
        g2T = vecs[:, KC:2 * KC]
        lngT = vecs[:, 2 * KC:2 * KC + NG]
        qg_raw = vecs[:, 2 * KC + NG:2 * KC + NG + 1]
        kg = vecs[:, 2 * KC + NG + 1:2 * KC + NG + 2]

        def act_op(out, in_, func, accum=None, scale=1.0, extra_reads=()):
            wr = [out] + ([accum] if accum is not None else [])
            rd = [in_] + list(extra_reads)
            if accum is not None:
                return S.op("act", lambda e: e.activation(out=out, in_=in_, func=func, scale=scale, accum_out=accum),
                            reads=rd, writes=wr)
            return S.op("act", lambda e: e.activation(out=out, in_=in_, func=func, scale=scale),
                        reads=rd, writes=wr)

        def ts_op(eng, out, in0, s1, s2, op0, op1=None):
            rd = [in0] + [s for s in (s1, s2) if not isinstance(s, (int, float, type(None)))]
            if op1 is None:
                return S.op(eng, lambda e: e.tensor_scalar(out=out, in0=in0, scalar1=s1, scalar2=None, op0=op0),
                            reads=rd, writes=[out])
            return S.op(eng, lambda e: e.tensor_scalar(out=out, in0=in0, scalar1=s1, scalar2=s2, op0=op0, op1=op1),
                        reads=rd, writes=[out])

        def tt_op(eng, out, in0, in1, op):
            return S.op(eng, lambda e: e.tensor_tensor(out=out, in0=in0, in1=in1, op=op),
                        reads=[in0, in1], writes=[out])

        def stt_op(eng, out, in0, scalar, in1, op0, op1):
            rd = [in0, in1] + ([scalar] if not isinstance(scalar, (int, float)) else [])
            return S.op(eng, lambda e: e.scalar_tensor_tensor(out=out, in0=in0, scalar=scalar, in1=in1, op0=op0, op1=op1),
                        reads=rd, writes=[out])

        def copy_op(out, in_):
            flip[0] ^= 1
            if flip[0]:
                return S.op("dve", lambda e: e.tensor_copy(out=out, in_=in_), reads=[in_], writes=[out])
            return act_op(out, in_, AF.Copy)

        def mm(out, lhsT, rhs, start, stop):
            return S.op("pe", lambda e: e.matmul(out, lhsT=lhsT, rhs=rhs, start=start, stop=stop),
                        reads=[lhsT, rhs], writes=[out])

        def rstd_op(out, in_, mult, eps):
            S.op("act", lambda e: e.activation(out=out, in_=in_, func=AF.Sqrt, scale=mult, bias=eps),
                 reads=[in_], writes=[out])
            S.op("dve", lambda e: e.reciprocal(out=out, in_=out), reads=[out], writes=[out])

        def load_piece(W, k0, k1, c0, w, pw=512):
            slot, rt = ring.next()
            if pw != 512:
                rt = rt.rearrange("p k n -> p (k n)").rearrange("p (k n) -> p k n", n=pw)
            dst = rt[:, 0:k1 - k0, 0:w]
            src = W[k0 * 128:k1 * 128, c0:c0 + w].rearrange("(kc p) n -> p kc n", p=128)
            S.dma("pool", dst, src, ("ring", slot))
            return rt

        def fm_job(W, Kc, c0, ncols, rhs_fn, evac, bw=512):
            nblk = (ncols + bw - 1) // bw
            for b in range(nblk):
                w = min(bw, ncols - b * bw)
                nm = w // 128
                ba = [banks.alloc() for _ in range(nm)]
                bk = [x[1] for x in ba]
                kstep = 4096 // bw
                for k0 in range(0, Kc, kstep):
                    k1 = min(Kc, k0 + kstep)
                    rt = load_piece(W, k0, k1, c0 + b * bw, w, pw=bw)
                    for m in range(nm):
                        for kc in range(k0, k1):
                            mm(bk[m], rt[:, kc - k0, m * 128:(m + 1) * 128], rhs_fn(kc), kc == 0, kc == Kc - 1)
                older = S.take_deferred()
                evac(b, bk, nm)
                for x in ba:
                    banks.release(x[0])
                for fn in older:
                    fn()

        def tm_job(W, Kc, c0, ncols, lhsT_fn, evac, kbase=0):
            nblk = (ncols + 511) // 512
            for b in range(nblk):
                w = min(512, ncols - b * 512)
                ba = [banks.alloc() for _ in range(4)]
                bk = [x[1] for x in ba]
                for k0 in range(0, Kc, 8):
                    k1 = min(Kc, k0 + 8)
                    rt = load_piece(W, kbase + k0, kbase + k1, c0 + b * 512, w)
                    for kc in range(k0, k1):
                        for t in range(4):
                            mm(bk[t][:, 0:w], lhsT_fn(kc, t), rt[:, kc - k0, 0:w], kc == 0, kc == Kc - 1)
                older = S.take_deferred()
                evac(b, bk, w)
                for x in ba:
                    banks.release(x[0])
                for fn in older:
                    fn()

        S.dma("sp", vecs[:, :], vecs_d[:, :], "const", closed=True)
        S.dma("pool", wsT[:, :], wsT_d[:, :], "constp", closed=True)
        S.op("pool", lambda e: e.memset(ident[:, :], 0.0), writes=[ident[:, :]])
        S.op("pool", lambda e: e.affine_select(out=ident[:, :], in_=ident[:, :], pattern=[[-1, 128]],
                                               compare_op=ALU.not_equal, fill=1.0, base=0, channel_multiplier=1),
             reads=[ident[:, :]], writes=[ident[:, :]])
        S.op("pool", lambda e: e.memset(ones[:, :], 1.0), writes=[ones[:, :]])
        ts_op("dve", qgs[:, :], qg_raw, float(128 ** -0.5), None, ALU.mult)
        assert GW <= NSCRF * 512
        lnb_bc = scrF_t[:, 0:GW]
        lnb_bf = xs[0][:, 0:GW]
        bs_bc = aview(B2_O, F32, [128, GW])
        S.dma("sp", lnb_bc, bass.AP(lnb_d.tensor, 0, [[0, 128], [1, GW]]), "const", closed=True)
        S.dma("sp", bs_bc, bass.AP(bs_d.tensor, 0, [[0, 128], [1, GW]]), "const", closed=True)
        S.op("dve", lambda e: e.tensor_copy(out=lnb_bf, in_=lnb_bc), reads=[lnb_bc], writes=[lnb_bf])
        for g0 in range(0, NG, 4):
            n = min(4, NG - g0)
            bi_, bk = banks.alloc()
            for j in range(n):
                gi = g0 + j
                mm(bk[:, j * 128:(j + 1) * 128], lnb_bf[:, gi * 128:(gi + 1) * 128], wsT[:, gi * 128:(gi + 1) * 128], True, True)
            tt_op("dve", bias2[:, g0 * 128:(g0 + n) * 128], bk[:, 0:n * 128], bs_bc[:, g0 * 128:(g0 + n) * 128], ALU.add)
            banks.release(bi_)

        def norm_front(src):
            xsb = xs_ring.next()
            ssq = smallc.next()
            rstd = smallc.next()
            act_op(xsb, src, AF.Square, accum=ssq)
            rstd_op(rstd, ssq, 1.0 / D, RMS_EPS)
            cs = (3 * D // 4) // 128 * 128
            ts_op("dve", xsb[:, 0:cs], src[:, 0:cs], rstd, None, ALU.mult)
            S.op("act", lambda e, o=xsb[:, cs:D], i=src[:, cs:D], r=rstd: e.activation(out=o, in_=i, func=AF.Copy, scale=r),
                 reads=[src[:, cs:D], rstd], writes=[xsb[:, cs:D]])
            return xsb

        def norm_tile(src, gT, dst, pos):
            norm_back(norm_front(src), gT, dst, pos)

        def norm_back(xsb, gT, dst, pos):
            for kc0 in range(0, KC, 8):
                n = min(8, KC - kc0)
                bi_, bk = banks.alloc()
                bk = bk.bitcast(BF16)
                for j in range(n):
                    o = bk[:, j * 128:(j + 1) * 128]
                    i_ = xsb[:, (kc0 + j) * 128:(kc0 + j + 1) * 128]
                    S.op("pe", lambda e, o=o, i_=i_: e.transpose(out=o, in_=i_, identity=ident[:, :]),
                         reads=[i_, ident[:, :]], writes=[o])
                src3 = bk[:, 0:n * 128].rearrange("p (a b) -> p a b", a=n)
                tt_op("dve", dst[:, kc0:kc0 + n, pos * 128:(pos + 1) * 128], src3,
                      bc_last(gT[:, kc0:kc0 + n], 128), ALU.mult)
                banks.release(bi_)

        def load_x_tile(row0):
            slot, xl = xl_ring.next()
            S.dma("sp", xl, x_d[row0:row0 + 128, :], ("xl", slot))
            return xl

        def evac_qk(bk, nm, h0, dests, gain):
            for m in range(nm):
                bank = bk[m]
                sq = scrB.next()
                raw = scrF_raw.next()
                act_op(raw, bank, AF.Copy)
                tt_op("dve", sq, raw, raw, ALU.mult)

                def pe_work(sq=sq, raw=raw, h=h0 + m):
                    bi_, ssb = banks.alloc()
                    mm(ssb, ones[:, :], sq, True, True)
                    rs = scrF_rs.next()
                    rstd_op(rs, ssb, 1.0 / 128, RMS_EPS)
                    banks.release(bi_)
                    for dst_fn, lo, hi in dests:
                        stt_op("dve", dst_fn(h), raw[:, lo:hi], gain, rs[:, lo:hi], ALU.mult, ALU.mult)
                S.defer(pe_work)

        def group(g):
            S.epoch = g + 1

            stored = [False]

            def stop_at(k):
                return getattr(cfg, "STOP", 99) <= k

            def finish():
                S.flush()
                if stored[0]:
                    return
                def stores(g=g):
                    for t in range(4):
                        r0 = g * T + t * 128
                        S.dma("act", out_d[r0:r0 + 128, :], h1[:, t, :], ("out", t))
                pending_out.append(stores)
            if stop_at(0):
                return finish()
            base = g * T
            xn_rhs = lambda kc: xnT[:, kc, :]
            xn_lhsT = lambda kc, t: xnT[:, kc, t * 128:(t + 1) * 128]

            hs_ = (0, 1, 6, 7)
            xls = [load_x_tile(base + hs_[0] * 128), load_x_tile(base + hs_[1] * 128)]
            while pending_out:
                pending_out.pop(0)()
            for pos in range(4):
                norm_tile(xls[pos], g1T, xnT, pos)
                if pos + 2 < 4:
                    xls.append(load_x_tile(base + hs_[pos + 2] * 128))
            if stop_at(1):
                return finish()
            own_front = {}
            for pos in range(2):
                own_front[pos] = norm_front(load_x_tile(base + (2 + pos) * 128))
            halo_dests = [(lambda h: KnT[:, h, 0:256], 0, 256), (lambda h: KnT[:, h, 768:1024], 256, 512)]
            fm_job(w_in, KC, cfg.K0, NW, xn_rhs,
                   lambda b, bk, nm: evac_qk(bk, nm, b * 2, halo_dests, kg), bw=256)
            hslots = (0, 1, 6, 7)

            def evac_v_halo(b, bk, w):
                for t in range(4):
                    copy_op(Vt[:, hslots[t], b * 512:b * 512 + w], bk[t][:, 0:w])
            tm_job(w_in, KC, cfg.VV0, NW, xn_lhsT, evac_v_halo)
            S.flush()

            if stop_at(2):
                return finish()
            for pos in range(4):
                norm_back(own_front[pos], g1T, xnT, pos)
                if pos + 2 < 4:
                    own_front[pos + 2] = norm_front(load_x_tile(base + (2 + pos + 2) * 128))
            own_dests = [(lambda h: KnT[:, h, 256:768], 0, 512)]
            fm_job(w_in, KC, cfg.K0, NW, xn_rhs,
                   lambda b, bk, nm: evac_qk(bk, nm, b * 2, own_dests, kg), bw=256)

            S.dma("pool", nmask.rearrange("p a b -> p (a b)"), nm_d[g * 128:(g + 1) * 128, :], "nmask")
            bt_pre = bt_ring.next()
            S.dma("pool", bt_pre[1].rearrange("p a b -> p (a b)"), bias_d[0:128, :], ("bt", bt_pre[0]))
            def evac_v_own(b, bk, w):
                for t in range(4):
                    copy_op(Vt[:, 2 + t, b * 512:b * 512 + w], bk[t][:, 0:w])
            tm_job(w_in, KC, cfg.VV0, NW, xn_lhsT, evac_v_own)
            q_dests = [(lambda h: QnT[:, h, :], 0, 512)]
            fm_job(w_in, KC, cfg.Q0, NW, xn_rhs,
                   lambda b, bk, nm: evac_qk(bk, nm, b * 2, q_dests, qgs[:, :]), bw=256)
            S.flush()

            if stop_at(3):
                return finish()
            pend = []

            def pv(h, s, pt, yba):
                yb = [x[1] for x in yba]
                qi = s - 2
                lo, hi = KB[s]
                nb = hi - lo
                col = (qi % 2) * 256
                ybank = yb[qi // 2]
                for jj in range(nb):
                    mm(ybank[:, col:col + 128], Vt[:, lo + jj, h * 128:(h + 1) * 128], pt[:, jj, :], jj == 0, jj == nb - 1)
                for jj in range(nb):
                    mm(ybank[:, col + 128:col + 256], ones[:, :], pt[:, jj, :], jj == 0, jj == nb - 1)

            def fin_head(h, yba):
                yb = [x[1] for x in yba]
                for x in yba:
                    banks.release(x[0])
                for half in range(2):
                    v3 = yb[half].rearrange("p (a b) -> p a b", a=2)
                    rd = scrF.next()[:, 0:256].rearrange("p (a b) -> p a b", a=2)
                    S.op("dve", lambda e, rd=rd, v3=v3: e.reciprocal(out=rd, in_=v3[:, :, 128:256]),
                         reads=[v3[:, :, 128:256]], writes=[rd])
                    dst = yBT[:, h, half * 256:(half + 1) * 256].rearrange("p (a b) -> p a b", a=2)
                    tt_op("dve", dst, v3[:, :, 0:128], rd, ALU.mult)

            for h in range(NH):
                if h == 0:
                    bslot, bt = bt_pre
                else:
                    bslot, bt = bt_ring.next()
                    S.dma("pool", bt.rearrange("p a b -> p (a b)"), bias_d[h * 128:(h + 1) * 128, :], ("bt", bslot))
                yb = [banks.alloc() for _ in range(2)]
                for s in (2, 3, 4, 5):
                    qi = s - 2
                    lo, hi = KB[s]
                    nb = hi - lo
                    sba = [banks.alloc() for _ in range(2)]
                    sb = [x[1] for x in sba]
                    for jj in range(nb):
                        j = lo + jj
                        o = sb[jj // 4][:, (jj % 4) * 128:(jj % 4 + 1) * 128]
                        mm(o, KnT[:, h, j * 128:(j + 1) * 128], QnT[:, h, qi * 128:(qi + 1) * 128], True, False)
                        mm(o, ident[:, :], bt[:, j - s + 3, :], False, False)
                        mm(o, ident[:, :], nmask[:, NM_OFF[s] + jj, :], False, True)
                    pt = pt_ring.next()
                    n0 = min(4, nb)
                    act_op(pt[:, 0:n0, :], sb[0][:, 0:n0 * 128].rearrange("p (a b) -> p a b", a=n0), AF.Exp)
                    if nb > 4:
                        act_op(pt[:, 4:nb, :], sb[1][:, 0:(nb - 4) * 128].rearrange("p (a b) -> p a b", a=nb - 4), AF.Exp)
                    for x in sba:
                        banks.release(x[0])
                    if pend:
                        a_ = pend.pop()
                        pv(*a_[:4])
                        if a_[4]:
                            fin_head(a_[0], a_[3])
                    pend.append((h, s, pt, yb, s == 5))
            a_ = pend.pop()
            pv(*a_[:4])
            fin_head(a_[0], a_[3])

            if stop_at(4):
                return finish()
            nvb = (GW + 511) // 512
            sums = [sums4.next() for _ in range(4)]

            def evac_va(b, bk, w):
                for t in range(4):
                    act_op(gv[:, t, b * 512:b * 512 + w], bk[t][:, 0:w], AF.Gelu, accum=sums[t][:, b:b + 1])
            tm_job(w_in, KC, cfg.V0, GW, xn_lhsT, evac_va)
            S.flush()
            for t in range(4):
                ssq = smallc.next()
                mean = smallc.next()
                var = smallc.next()
                tot = smallc.next()
                act_op(zt[:, t, :], gv[:, t, :], AF.Square, accum=ssq)
                S.op("dve", lambda e, tot=tot, s_=sums[t]: e.reduce_sum(out=tot, in_=s_[:, 0:nvb], axis=mybir.AxisListType.X),
                     reads=[sums[t][:, 0:nvb]], writes=[tot])
                ts_op("dve", mean, tot, 1.0 / GW, None, ALU.mult)
                tt_op("dve", var, mean, mean, ALU.mult)
                stt_op("dve", var, ssq, 1.0 / GW, var, ALU.mult, ALU.subtract)
                rstd_op(var, var, 1.0, LN_EPS)
                ts_op("dve", zt[:, t, :], gv[:, t, :], mean, var, ALU.subtract, ALU.mult)
            def evac_u(b, bk, nm):
                for m in range(nm):
                    act_op(uT[:, b * 4 + m, :], bk[m], AF.Gelu)
            fm_job(w_in, KC, cfg.U0, GW, xn_rhs, evac_u)
            for t in range(4):
                for g0 in range(0, NG, 4):
                    n = min(4, NG - g0)
                    bi_, bk = banks.alloc()
                    for j in range(n):
                        gi = g0 + j
                        mm(bk[:, j * 128:(j + 1) * 128], zt[:, t, gi * 128:(gi + 1) * 128], wsT[:, gi * 128:(gi + 1) * 128], True, True)
                    tmp = scrF.next()
                    for j in range(n):
                        gi = g0 + j
                        stt_op("dve", tmp[:, j * 128:(j + 1) * 128], bk[:, j * 128:(j + 1) * 128], lngT[:, gi:gi + 1],
                               bias2[:, gi * 128:(gi + 1) * 128], ALU.mult, ALU.add)
                    tt_op("dve", yAT[:, g0:g0 + n, t * 128:(t + 1) * 128],
                          tmp[:, 0:n * 128].rearrange("p (a b) -> p a b", a=n),
                          uT[:, g0:g0 + n, t * 128:(t + 1) * 128], ALU.mult)
                    banks.release(bi_)

            if stop_at(5):
                return finish()
            ya_rhs = lambda kc: yAT[:, kc, :]
            yb_rhs = lambda kc: yBT[:, kc, :]
            for cb in range(D // 512):
                sa = mscr[0:4]

                def evac_ga(b, bk, nm, sa=sa):
                    for m in range(nm):
                        act_op(sa[m], bk[m], AF.Sigmoid)
                fm_job(w_in, KC, cfg.GA0 + cb * 512, 512, xn_rhs, evac_ga)

                def evac_a(b, bk, nm, sa=sa):
                    for m in range(nm):
                        tt_op("dve", sa[m], bk[m], sa[m], ALU.mult)
                fm_job(w_oa, NG, cb * 512, 512, ya_rhs, evac_a)
                sbb = mscr[4:8]

                def evac_gb(b, bk, nm, sbb=sbb):
                    for m in range(nm):
                        act_op(sbb[m], bk[m], AF.Sigmoid)
                fm_job(w_in, KC, cfg.GB0 + cb * 512, 512, xn_rhs, evac_gb)

                def evac_b(b, bk, nm, sa=sa, sbb=sbb, cb=cb):
                    for m in range(nm):
                        tt_op("dve", sbb[m], bk[m], sbb[m], ALU.mult)
                        tt_op("dve", mergedT[:, cb * 4 + m, :], sa[m], sbb[m], ALU.add)
                fm_job(w_ob, NH, cb * 512, 512, yb_rhs, evac_b)

            if stop_at(6):
                return finish()
            own_row0 = base + 256

            def evac_out(b, bk, w):
                slot, xr = xres_ring.next()
                src = x_d[own_row0:own_row0 + 512, b * 512:b * 512 + w].rearrange("(t p) n -> p t n", p=128)
                S.dma("sp", xr[:, :, 0:w], src, ("xres", slot))
                for t in range(4):
                    tt_op("dve", h1[:, t, b * 512:b * 512 + w], bk[t][:, 0:w], xr[:, t, 0:w], ALU.add)
            tm_job(w_out, KC, 0, D, lambda kc, t: mergedT[:, kc, t * 128:(t + 1) * 128], evac_out)

            if stop_at(7):
                return finish()
            for t in range(4):
                norm_tile(h1[:, t, :], g2T, xnT, t)

            if stop_at(8):
                return finish()
            FC = cfg.FC
            cap = (min(ACT_CAP, getattr(cfg, 'ACT_CAP_OVERRIDE', ACT_CAP)) // 4) * 4
            nparts = (FC + cap - 1) // cap
            per = ((FC + nparts - 1) // nparts + 3) // 4 * 4
            c_lo = 0
            while c_lo < FC:
                c_hi = min(FC, c_lo + per)
                nch = c_hi - c_lo
                for cb0 in range(c_lo, c_hi, 4):
                    n4 = min(4, c_hi - cb0)
                    sg = [scrF.next() for _ in range(n4)]

                    def evac_gate(b, bk, nm, sg=sg):
                        for m in range(nm):
                            act_op(sg[m], bk[m], AF.Silu)
                    fm_job(w_g, KC, cb0 * 128, n4 * 128, xn_rhs, evac_gate)

                    def evac_up(b, bk, nm, sg=sg, cb0=cb0, c_lo=c_lo):
                        for m in range(nm):
                            tt_op("dve", actT[:, cb0 - c_lo + m, :], bk[m], sg[m], ALU.mult)
                    fm_job(w_u, KC, cb0 * 128, n4 * 128, xn_rhs, evac_up)

                last_part = c_hi == FC

                def evac_down(b, bk, w, last_part=last_part):
                    for t in range(4):
                        tt_op("dve", h1[:, t, b * 512:b * 512 + w], bk[t][:, 0:w], h1[:, t, b * 512:b * 512 + w], ALU.add)
                    if last_part:
                        dst = out_d[g * T:g * T + 512, b * 512:b * 512 + w].rearrange("(t p) n -> p t n", p=128)
                        S.dma("act", dst, h1[:, :, b * 512:b * 512 + w], ("out", b))
                        stored[0] = True
                tm_job(w_d, nch, 0, D, lambda kc, t: actT[:, kc, t * 128:(t + 1) * 128], evac_down, kbase=c_lo)
                c_lo = c_hi

            return finish()

        pending_out = []
        for g in range(cfg.NGRP):
            group(g)
        while pending_out:
            pending_out.pop(0)()
        S.out_dmas = [k for k in S.dma_cum if isinstance(k, tuple) and k[0] == "out"]

        S.prepare(nc, stack)
        with nc.Block() as block:
            S.emit(block)
    return nc, S


def _bias_table(rpb):
    NH = rpb.shape[0]
    k = np.arange(128)
    q = np.arange(128)
    krl, kc = k // 64, k % 64
    qrl, qc = q // 64, q % 64
    tab = np.zeros((NH, 128, 7, 128), np.float32)
    for ri, rho in enumerate(range(-3, 4)):
        dr = 2 * rho + krl[:, None] - qrl[None, :]
        dc = kc[:, None] - qc[None, :]
        ok = (np.abs(dr) <= 7) & (np.abs(dc) <= 15)
        ir = np.clip(dr + 7, 0, 14)
        ic = np.clip(dc + 15, 0, 30)
        vals = rpb[:, ir, ic]
        tab[:, :, ri, :] = np.where(ok[None], vals, np.float32(0.0))
    return np.ascontiguousarray(tab.reshape(NH * 128, 7 * 128))


def _neg_mask(cfg, core):
    ROWS = cfg.ROWS
    out = np.full((cfg.NGRP, 128, NM_TILES, 128), NEG, np.float32)
    k = np.arange(128)
    q = np.arange(128)
    krl, kcc = k // 64, k % 64
    qrl, qcc = q // 64, q % 64
    cs = np.clip(qcc - 8, 0, GRID_W - 16)
    for g in range(cfg.NGRP):
        for s in (2, 3, 4, 5):
            tq = core * cfg.TPC + 4 * g + (s - 2)
            r = 2 * tq + qrl
            rs = np.clip(r - 4, 0, ROWS - 8)
            cnt = np.zeros(128, np.int64)
            lo, hi = KB[s]
            for jj, j in enumerate(range(lo, hi)):
                tk = core * cfg.TPC + 4 * g + (j - 2)
                kr = 2 * tk + krl
                ok = (kr[:, None] >= 0) & (kr[:, None] < ROWS) \
                    & (kr[:, None] >= rs[None, :]) & (kr[:, None] < rs[None, :] + 8) \
                    & (kcc[:, None] >= cs[None, :]) & (kcc[:, None] < cs[None, :] + 16)
                out[g, :, NM_OFF[s] + jj, :] = np.where(ok, np.float32(0.0), np.float32(NEG))
                cnt += ok.sum(0)
            assert (cnt == 128).all(), (core, g, s, cnt)
    return np.ascontiguousarray(out.reshape(cfg.NGRP * 128, NM_TILES * 128))


_CACHE = {}


def run(cfg, x, norm1_g, w_in, gm_ln_g, gm_ln_b, gm_w_s, gm_b_s, q_gain, k_gain, na_rpb,
        w_o_gm, w_o_na, w_out, norm2_g, w_ff_gate, w_ff_up, w_ff_down, trace=False):
    f = lambda a: np.ascontiguousarray(np.asarray(a, np.float32))
    D, KC, NG = cfg.D, cfg.KC, cfg.NG
    x2 = f(x).reshape(cfg.SEQ, D)
    xpad = np.zeros((cfg.SEQ + 512, D), np.float32)
    xpad[256:256 + cfg.SEQ] = x2
    vecs = np.concatenate([
        f(norm1_g).reshape(KC, 128).T, f(norm2_g).reshape(KC, 128).T,
        f(gm_ln_g).reshape(NG, 128).T, f(q_gain).reshape(1, 128).T, f(k_gain).reshape(1, 128).T], axis=1)
    shared = {
        "w_in": f(w_in).reshape(D, cfg.IN_COLS), "w_o_gm": f(w_o_gm).reshape(cfg.GW, D),
        "w_o_na": f(w_o_na).reshape(cfg.NW, D), "w_out": f(w_out).reshape(D, D),
        "w_ff_gate": f(w_ff_gate).reshape(D, cfg.DFF), "w_ff_up": f(w_ff_up).reshape(D, cfg.DFF),
        "w_ff_down": f(w_ff_down).reshape(cfg.DFF, D),
        "vecs": np.ascontiguousarray(vecs),
        "lnb_row": f(gm_ln_b).reshape(1, cfg.GW), "bs_row": f(gm_b_s).reshape(1, cfg.GW),
        "wsT": np.ascontiguousarray(f(gm_w_s).reshape(NG, 128, 128).transpose(2, 0, 1).reshape(128, cfg.GW)),
        "biasT": _bias_table(f(na_rpb).reshape(cfg.NH, 15, 31)),
    }
    if getattr(cfg, "TINYW", False):
        for k in ("w_in", "w_o_gm", "w_o_na", "w_out", "w_ff_gate", "w_ff_up", "w_ff_down"):
            shared[k] = np.zeros((128, 128), np.float32)
    in_maps = []
    for c in range(cfg.NCORES):
        m = dict(shared)
        m["x"] = xpad[c * cfg.TOK: c * cfg.TOK + cfg.TOK + 512]
        m["negmask"] = _neg_mask(cfg, c)
        in_maps.append(m)
    key = (cfg.D, cfg.SEQ, cfg.DFF)
    if key not in _CACHE:
        _CACHE[key] = build(cfg)[0]
    nc = _CACHE[key]
    res = run_bass_kernel_spmd(nc, in_maps, core_ids=list(range(cfg.NCORES)), trace=trace)
    out = np.concatenate([res.results[c]["out"] for c in range(cfg.NCORES)], axis=0)
    return out.reshape(1, cfg.SEQ, D).astype(np.float32, copy=False), res


def kernel(**inputs):
    cfg = Cfg()
    out, _ = run(cfg, **inputs)
    return out
```
